# Optimizing a Trainium2 kernel written in Bass

```python
import jax, jax.numpy as jnp
from jax import lax
import numpy as np

D_MODEL = 2048
BATCH = 4
SEQ = 8192
DEPTH = 4

RMS_EPS = 1e-5

HEAD_DIM = 64
SWA_Q_HEADS = (D_MODEL // 2) // HEAD_DIM
SWA_GQA_RATIO = 4
SWA_KV_HEADS = SWA_Q_HEADS // SWA_GQA_RATIO
SWA_GROUP = SWA_Q_HEADS // SWA_KV_HEADS
WINDOW = 128
BLOCK = 128

MLSTM_HEADS = 4
MLSTM_V_DIM = (D_MODEL // 2) // MLSTM_HEADS
MLSTM_QK_DIM = MLSTM_V_DIM // 2
MLSTM_CHUNK = 128
GATE_SOFTCAP = 15.0

FOX_HEAD_DIM = 128
FOX_HEADS = D_MODEL // FOX_HEAD_DIM
FOX_BLOCK = 128

FFN_HIDDEN = -(-(8 * D_MODEL) // (3 * 256)) * 256

EVEN_SIZES = (SWA_Q_HEADS * HEAD_DIM, SWA_KV_HEADS * HEAD_DIM, SWA_KV_HEADS * HEAD_DIM,
              MLSTM_HEADS * MLSTM_QK_DIM, MLSTM_HEADS * MLSTM_QK_DIM,
              MLSTM_HEADS * MLSTM_V_DIM, MLSTM_HEADS * MLSTM_V_DIM,
              MLSTM_HEADS, MLSTM_HEADS)
EVEN_IN = sum(EVEN_SIZES)
EVEN_MIX = SWA_Q_HEADS * HEAD_DIM + MLSTM_HEADS * MLSTM_V_DIM
FOX_SIZES = (FOX_HEADS * FOX_HEAD_DIM, FOX_HEADS * FOX_HEAD_DIM, FOX_HEADS * FOX_HEAD_DIM, FOX_HEADS)
FOX_IN = sum(FOX_SIZES)
FOX_MIX = FOX_HEADS * FOX_HEAD_DIM

kernel_name = "hybrid_swa_mlstm_fox_trunk"


def rms_norm(x, g):
    xf = x.astype(jnp.float32)
    y = xf * lax.rsqrt(jnp.mean(xf * xf, axis=-1, keepdims=True) + RMS_EPS)
    return (y * g.astype(jnp.float32)).astype(x.dtype)


def split_cols(t, sizes):
    out, start = [], 0
    for s in sizes:
        out.append(t[..., start:start + s])
        start += s
    return out


def softcap(t):
    return GATE_SOFTCAP * jnp.tanh(t / GATE_SOFTCAP)


def alibi_slopes(n):
    return 2.0 ** (-8.0 * jnp.arange(1, n + 1, dtype=jnp.float32) / n)


def sliding_window_attention(q, k, v, sinks):
    B, S = q.shape[0], q.shape[1]
    nb = S // BLOCK
    qb = q.reshape(B, nb, BLOCK, SWA_KV_HEADS, SWA_GROUP, HEAD_DIM)

    def band(t):
        tb = t.reshape(B, nb, BLOCK, SWA_KV_HEADS, HEAD_DIM)
        prev = jnp.concatenate([jnp.zeros_like(tb[:, :1]), tb[:, :-1]], axis=1)
        return jnp.concatenate([prev, tb], axis=2)

    kb, vb = band(k), band(v)
    qi = jnp.arange(BLOCK)[:, None] + BLOCK
    kj = jnp.arange(2 * BLOCK)[None, :]
    dist = qi - kj
    key_abs = (jnp.arange(nb)[:, None, None] - 1) * BLOCK + kj
    valid = (dist >= 0) & (dist < WINDOW) & (key_abs >= 0)
    slopes = alibi_slopes(SWA_Q_HEADS).reshape(SWA_KV_HEADS, SWA_GROUP, 1, 1)
    scores = jnp.einsum('bnqhgd,bnkhd->bnhgqk', qb, kb,
                        preferred_element_type=jnp.float32) * (HEAD_DIM ** -0.5)
    scores = scores - slopes * dist.astype(jnp.float32)
    scores = jnp.where(valid[:, None, None], scores, -jnp.inf)
    sink = sinks.astype(jnp.float32).reshape(SWA_KV_HEADS, SWA_GROUP, 1, 1)
    m = jnp.maximum(jnp.max(scores, axis=-1, keepdims=True), sink)
    p = jnp.exp(scores - m)
    probs = p / (jnp.sum(p, axis=-1, keepdims=True) + jnp.exp(sink - m))
    out = jnp.einsum('bnhgqk,bnkhd->bnqhgd', probs.astype(v.dtype), vb)
    return out.reshape(B, S, SWA_Q_HEADS * HEAD_DIM)


def mlstm_chunkwise(q, k, v, i_pre, f_pre):
    B, S = q.shape[0], q.shape[1]
    L = MLSTM_CHUNK
    nc = S // L
    H = MLSTM_HEADS
    f32 = jnp.float32

    def chunks(t):
        return jnp.transpose(t.reshape(B, nc, L, H, -1), (0, 3, 1, 2, 4)).astype(f32)

    def gchunks(t):
        return jnp.transpose(t.reshape(B, nc, L, H), (0, 3, 1, 2))

    qc = chunks(q) * (MLSTM_QK_DIM ** -0.5)
    kc, vc = chunks(k), chunks(v)
    log_i = gchunks(i_pre)
    log_f = jax.nn.log_sigmoid(gchunks(f_pre))
    b = jnp.cumsum(log_f, axis=-1)
    b_last = b[..., -1]

    a = b_last[..., None] - b + log_i
    a_max = jnp.max(a, axis=-1)
    w = jnp.exp(a - a_max[..., None])
    dC = jnp.einsum('bhcl,bhcld,bhcle->bhcde', w, kc, vc)
    dn = jnp.einsum('bhcl,bhcld->bhcd', w, kc)

    def step(carry, xs):
        C, n, m = carry
        dC_c, dn_c, bl, am = xs
        m_new = jnp.maximum(bl + m, am)
        decay = jnp.exp(bl + m - m_new)
        scale = jnp.exp(am - m_new)
        C_new = decay[..., None, None] * C + scale[..., None, None] * dC_c
        n_new = decay[..., None] * n + scale[..., None] * dn_c
        return (C_new, n_new, m_new), (C, n, m)

    init = (jnp.zeros((B, H, MLSTM_QK_DIM, MLSTM_V_DIM), f32),
            jnp.zeros((B, H, MLSTM_QK_DIM), f32),
            jnp.zeros((B, H), f32))
    xs = (jnp.moveaxis(dC, 2, 0), jnp.moveaxis(dn, 2, 0),
          jnp.moveaxis(b_last, 2, 0), jnp.moveaxis(a_max, 2, 0))
    _, (C_prev, n_prev, m_prev) = lax.scan(step, init, xs)
    C_prev = jnp.moveaxis(C_prev, 0, 2)
    n_prev = jnp.moveaxis(n_prev, 0, 2)
    m_prev = jnp.moveaxis(m_prev, 0, 2)

    causal = jnp.tril(jnp.ones((L, L), dtype=bool))
    log_D = b[..., :, None] - b[..., None, :] + log_i[..., None, :]
    log_D = jnp.where(causal, log_D, -jnp.inf)
    g = b + m_prev[..., None]
    m_t = jnp.maximum(g, jnp.max(log_D, axis=-1))
    D = jnp.exp(log_D - m_t[..., None])
    inter = jnp.exp(g - m_t)
    qk = jnp.einsum('bhctd,bhcsd->bhcts', qc, kc) * D
    num = jnp.einsum('bhcts,bhcse->bhcte', qk, vc) + \
        inter[..., None] * jnp.einsum('bhctd,bhcde->bhcte', qc, C_prev)
    den = jnp.sum(qk, axis=-1) + inter * jnp.einsum('bhctd,bhcd->bhct', qc, n_prev)
    h = num / jnp.maximum(jnp.abs(den), jnp.exp(-m_t))[..., None]
    return jnp.transpose(h, (0, 2, 3, 1, 4)).reshape(B, S, H, MLSTM_V_DIM)


def swa_mlstm_mixer(xn, w_in, w_out, sinks, b_i, b_f, head_gain):
    B, S = xn.shape[0], xn.shape[1]
    proj = xn @ w_in
    q_s, k_s, v_s, q_m, k_m, v_m, o_m, i_m, f_m = split_cols(proj, EVEN_SIZES)
    attn = sliding_window_attention(q_s.reshape(B, S, SWA_Q_HEADS, HEAD_DIM),
                                    k_s.reshape(B, S, SWA_KV_HEADS, HEAD_DIM),
                                    v_s.reshape(B, S, SWA_KV_HEADS, HEAD_DIM), sinks)
    i_pre = softcap(i_m.astype(jnp.float32) + b_i.astype(jnp.float32))
    f_pre = softcap(f_m.astype(jnp.float32) + b_f.astype(jnp.float32))
    h = mlstm_chunkwise(q_m.reshape(B, S, MLSTM_HEADS, MLSTM_QK_DIM),
                        k_m.reshape(B, S, MLSTM_HEADS, MLSTM_QK_DIM),
                        v_m.reshape(B, S, MLSTM_HEADS, MLSTM_V_DIM), i_pre, f_pre)
    h = rms_norm(h, head_gain.reshape(MLSTM_HEADS, MLSTM_V_DIM))
    o = jax.nn.sigmoid(o_m.astype(jnp.float32)).reshape(B, S, MLSTM_HEADS, MLSTM_V_DIM)
    mem = (o * h).astype(xn.dtype).reshape(B, S, MLSTM_HEADS * MLSTM_V_DIM)
    return jnp.concatenate([attn.astype(xn.dtype), mem], axis=-1) @ w_out


def forgetting_attention_mixer(xn, w_in, b_f, w_out):
    B, S = xn.shape[0], xn.shape[1]
    nb = S // FOX_BLOCK
    proj = xn @ w_in
    q, k, v, f = split_cols(proj, FOX_SIZES)
    to_heads = lambda t: jnp.transpose(t.reshape(B, S, FOX_HEADS, FOX_HEAD_DIM), (0, 2, 1, 3))
    q, k, v = to_heads(q), to_heads(k), to_heads(v)
    log_f = jax.nn.log_sigmoid(f.astype(jnp.float32) + b_f.astype(jnp.float32))
    c = jnp.transpose(jnp.cumsum(log_f, axis=1), (0, 2, 1))
    q_blocks = jnp.moveaxis(q.reshape(B, FOX_HEADS, nb, FOX_BLOCK, FOX_HEAD_DIM), 2, 0)
    c_blocks = jnp.moveaxis(c.reshape(B, FOX_HEADS, nb, FOX_BLOCK), 2, 0)
    pos_blocks = jnp.arange(S).reshape(nb, FOX_BLOCK)
    k_pos = jnp.arange(S)
    scale = FOX_HEAD_DIM ** -0.5

    def one_block(args):
        qb, cb, pb = args
        s = jnp.einsum('bhqd,bhkd->bhqk', qb, k, preferred_element_type=jnp.float32) * scale
        s = s + cb[..., None] - c[:, :, None, :]
        s = jnp.where(pb[:, None] >= k_pos[None, :], s, -jnp.inf)
        p = jax.nn.softmax(s, axis=-1)
        return jnp.einsum('bhqk,bhkd->bhqd', p.astype(v.dtype), v)

    out = lax.map(one_block, (q_blocks, c_blocks, pos_blocks))
    out = jnp.transpose(out, (1, 0, 3, 2, 4)).reshape(B, S, FOX_MIX)
    return out @ w_out


def swiglu(xn, w_gate, w_up, w_down):
    return (jax.nn.silu(xn @ w_gate) * (xn @ w_up)) @ w_down


def setup_inputs(seed: int = 0) -> dict:
    key = jax.random.key(seed)
    ks = jax.random.split(key, 18)
    f32 = jnp.float32
    n_even = (DEPTH + 1) // 2
    n_odd = DEPTH // 2
    nrm = lambda k, shape, s: s * jax.random.normal(k, shape, f32)
    return {
        'x': nrm(ks[0], (BATCH, SEQ, D_MODEL), 1.0),
        'norm_mix': 1.0 + nrm(ks[1], (DEPTH, D_MODEL), 0.02),
        'norm_ffn': 1.0 + nrm(ks[2], (DEPTH, D_MODEL), 0.02),
        'norm_final': 1.0 + nrm(ks[3], (D_MODEL,), 0.02),
        'w_in_even': nrm(ks[4], (n_even, D_MODEL, EVEN_IN), D_MODEL ** -0.5),
        'w_out_even': nrm(ks[5], (n_even, EVEN_MIX, D_MODEL), EVEN_MIX ** -0.5),
        'swa_sinks': nrm(ks[6], (n_even, SWA_Q_HEADS), 0.5),
        'mlstm_b_i': nrm(ks[7], (n_even, MLSTM_HEADS), 0.1),
        'mlstm_b_f': jax.random.uniform(ks[8], (n_even, MLSTM_HEADS), f32, 3.0, 6.0),
        'mlstm_head_gain': 1.0 + nrm(ks[9], (n_even, MLSTM_HEADS * MLSTM_V_DIM), 0.02),
        'w_in_odd': nrm(ks[10], (n_odd, D_MODEL, FOX_IN), D_MODEL ** -0.5),
        'fox_b_f': jax.random.uniform(ks[11], (n_odd, FOX_HEADS), f32, 1.0, 4.0),
        'w_out_odd': nrm(ks[12], (n_odd, FOX_MIX, D_MODEL), FOX_MIX ** -0.5),
        'w_gate': nrm(ks[13], (DEPTH, D_MODEL, FFN_HIDDEN), D_MODEL ** -0.5),
        'w_up': nrm(ks[14], (DEPTH, D_MODEL, FFN_HIDDEN), D_MODEL ** -0.5),
        'w_down': nrm(ks[15], (DEPTH, FFN_HIDDEN, D_MODEL), FFN_HIDDEN ** -0.5),
    }


def reference(x, norm_mix, norm_ffn, norm_final, w_in_even, w_out_even, swa_sinks,
              mlstm_b_i, mlstm_b_f, mlstm_head_gain, w_in_odd, fox_b_f, w_out_odd,
              w_gate, w_up, w_down):
    h = x
    for layer in range(DEPTH):
        j = layer // 2
        xn = rms_norm(h, norm_mix[layer])
        if layer % 2 == 0:
            mix = swa_mlstm_mixer(xn, w_in_even[j], w_out_even[j], swa_sinks[j],
                                  mlstm_b_i[j], mlstm_b_f[j], mlstm_head_gain[j])
        else:
            mix = forgetting_attention_mixer(xn, w_in_odd[j], fox_b_f[j], w_out_odd[j])
        h = h + mix.astype(h.dtype)
        h = h + swiglu(rms_norm(h, norm_ffn[layer]), w_gate[layer], w_up[layer], w_down[layer])
    return rms_norm(h, norm_final)
```

```python
import contextlib
import numpy as np
import concourse.bass as bass
import concourse.mybir as mybir
from concourse.bass_utils import run_bass_kernel_spmd

F32 = mybir.dt.float32
BF16 = mybir.dt.bfloat16
AF = mybir.ActivationFunctionType
ALU = mybir.AluOpType
AX = mybir.AxisListType

ENGS = ("tensor", "vector", "scalar", "gpsimd", "sync")


class Buf:
    __slots__ = ("name", "w", "r", "dsem", "dcnt", "dlast")

    def __init__(self, name=""):
        self.name = name
        self.w = None
        self.r = {}
        self.dsem = None
        self.dcnt = 0
        self.dlast = None


class Prog:
    def __init__(self, nc):
        self.nc = nc
        self.ops = {e: [] for e in ENGS}
        self.cnt = {e: 0 for e in ENGS}
        self.ndsem = 0
        self.final_waits = []

    def _deps(self, reads, writes):
        waits = {}
        def add(ev):
            if ev is None:
                return
            k, v = ev
            if waits.get(k, 0) < v:
                waits[k] = v
        for b in reads:
            add(b.w)
        for b in writes:
            add(b.w)
            for k, v in b.r.items():
                add((k, v))
        return waits

    def _commit(self, ev, reads, writes):
        k, v = ev
        for b in reads:
            if b.r.get(k, 0) < v:
                b.r[k] = v
        for b in writes:
            b.w = ev
            b.r = {}

    def op(self, eng, fn, reads=(), writes=()):
        waits = self._deps(reads, writes)
        self.cnt[eng] += 1
        ev = (eng, self.cnt[eng])
        self.ops[eng].append((fn, waits, ev, 1))
        self._commit(ev, reads, writes)
        return ev

    def dma(self, eng, fns, owner, reads=(), writes=()):
        if owner.dsem is None:
            owner.dsem = "d%d" % self.ndsem
            self.ndsem += 1
        waits = self._deps(reads, writes)
        if owner.dlast is not None:
            k, v = owner.dlast
            if waits.get(k, 0) < v:
                waits[k] = v
        ev = None
        for i, fn in enumerate(fns):
            owner.dcnt += 16
            ev = (owner.dsem, owner.dcnt)
            self.ops[eng].append((fn, waits if i == 0 else {}, ev, 16))
        owner.dlast = ev
        self._commit(ev, reads, writes)
        return ev

    def finish(self, evs):
        self.final_waits += [e for e in evs if e is not None]

    def emit(self):
        nc = self.nc
        import contextlib
        with contextlib.ExitStack() as st:
            sems = {}
            for e in ENGS:
                sems[e] = st.enter_context(nc.semaphore("s_" + e))
            for i in range(self.ndsem):
                sems["d%d" % i] = st.enter_context(nc.semaphore("sd%d" % i))
            block = st.enter_context(nc.Block())
            final = {}
            for k, v in self.final_waits:
                if final.get(k, 0) < v:
                    final[k] = v

            def run(engname, eng):
                waited = {}
                for fn, waits, ev, inc in self.ops[engname]:
                    for k, v in waits.items():
                        if k == "tensor" and engname == "tensor":
                            continue
                        if waited.get(k, 0) < v:
                            eng.wait_ge(sems[k], v)
                            waited[k] = v
                    ins = fn(eng)
                    ins.then_inc(sems[ev[0]], inc)
                if engname == "sync":
                    for k, v in final.items():
                        if waited.get(k, 0) < v:
                            eng.wait_ge(sems[k], v)

            block.sync(lambda e: run("sync", e))
            block.scalar(lambda e: run("scalar", e))
            block.vector(lambda e: run("vector", e))
            block.gpsimd(lambda e: run("gpsimd", e))
            block.tensor(lambda e: run("tensor", e))


D = 2048
KC = 16
HID = 5632
HC = 44
EPS = 1e-5


def build_B(NT, T=512, HG=1, last=False, NS=4, has_mix=True, do_ffn=True, write_h=True):
    nc = bass.Bass("TRN2", target_bir_lowering=False)
    RT = T // 128
    TT = T // 512
    HCG = HC // HG
    ntiles = NT // T
    h_in = nc.dram_tensor("h_in", [NT, D], F32, kind="ExternalInput").ap()
    if has_mix:
        mixT = nc.dram_tensor("mixT", [D, NT], BF16, kind="ExternalInput").ap()
        wo = nc.dram_tensor("wo", [D, D], F32, kind="ExternalInput").ap()
    g_ffn = nc.dram_tensor("g_ffn", [D], F32, kind="ExternalInput").ap()
    if do_ffn:
        wg = nc.dram_tensor("wg", [D, HID], F32, kind="ExternalInput").ap()
        wu = nc.dram_tensor("wu", [D, HID], F32, kind="ExternalInput").ap()
        wd = nc.dram_tensor("wd", [HID, D], F32, kind="ExternalInput").ap()
    g_next = nc.dram_tensor("g_next", [D], F32, kind="ExternalInput").ap()
    ident_d = nc.dram_tensor("ident", [128, 128], BF16, kind="ExternalInput").ap()
    if write_h or last:
        h_out = nc.dram_tensor("h_out", [NT, D], F32, kind="ExternalOutput").ap()
    if not last:
        xnT_out = nc.dram_tensor("xnT_out", [D, NT], BF16, kind="ExternalOutput").ap()

    P = Prog(nc)
    st = contextlib.ExitStack()
    with st:
        sb = lambda name, shape, dt: st.enter_context(nc.sbuf_tensor(name, shape, dt))
        h32 = sb("h32", [128, RT, D], F32)
        xnT = sb("xnT", [128, KC, T], BF16)
        GT = sb("GT", [128, HCG, T], BF16)
        ring = [sb("ring%d" % i, [128, 8192], BF16) for i in range(NS)]
        gf = sb("gf", [128, D], F32)
        gn = sb("gn", [128, D], F32)
        xn_tm = [sb("xn_tm%d" % i, [128, D], BF16) for i in range(2)]
        sg = [sb("sg%d" % i, [128, 512], F32) for i in range(2)]
        ident = sb("identsb", [128, 128], BF16)
        ss = sb("ss", [128, 8], F32)
        epsc = sb("epsc", [128, 1], F32)
        Beps = Buf()
        rstd = sb("rstd", [128, 8], F32)
        psall = st.enter_context(nc.psum_tensor("psall", [128, 4096], F32))
        psm = [psall[:, i * 512:(i + 1) * 512] for i in range(7)]
        pst = psall[:, 7 * 512:8 * 512].bitcast(BF16)

        Bh = [[Buf() for _ in range(4)] for _ in range(RT)]
        BxnT = [Buf() for _ in range(RT)]
        BGT = [[Buf() for _ in range(TT)] for _ in range(HCG)]
        Bring = [Buf() for _ in range(NS)]
        Bgf, Bgn, Bid = Buf(), Buf(), Buf()
        Bxtm = [Buf(), Buf()]
        Bsg = [Buf(), Buf()]
        Bss = [Buf() for _ in range(8)]
        Bps = [Buf() for _ in range(7)]
        Bpst = Buf()
        st_state = dict(ring=0, acc=0, gu=0, xtm=0, ssi=0)

        def next_ring():
            i = st_state["ring"] % NS
            st_state["ring"] += 1
            return i

        ACC = [0, 1, 2]
        GU = [(3, 4), (5, 6)]

        def next_acc():
            i = ACC[st_state["acc"] % len(ACC)]
            st_state["acc"] += 1
            return i

        P.op("vector", lambda e: e.memset(epsc[:], EPS), writes=[Beps])
        P.dma("sync", [lambda e: e.dma_start(out=gf[:], in_=g_ffn.partition_broadcast(128))], Bgf, writes=[Bgf])
        P.dma("sync", [lambda e: e.dma_start(out=gn[:], in_=g_next.partition_broadcast(128))], Bgn, writes=[Bgn])
        P.dma("sync", [lambda e: e.dma_start(out=ident[:], in_=ident_d)], Bid, writes=[Bid])

        Bssall = Buf()

        def rms_stats():
            for r in range(RT):
                xi = st_state["xtm"] % 2
                st_state["xtm"] += 1
                P.op("scalar", lambda e, r=r, xi=xi: e.activation(out=xn_tm[xi][:], in_=h32[:, r, :], func=AF.Square,
                                                                  accum_out=ss[:, r:r + 1]),
                     reads=Bh[r], writes=[Bxtm[xi], Bssall])
            P.op("scalar", lambda e: e.activation(out=rstd[:, 0:RT], in_=ss[:, 0:RT], func=AF.Sqrt,
                                                  scale=1.0 / D, bias=epsc[:, 0:1]),
                 reads=[Beps], writes=[Bssall])
            P.op("vector", lambda e: e.reciprocal(out=rstd[:, 0:RT], in_=rstd[:, 0:RT]),
                 reads=[], writes=[Bssall])

        def rmsnorm_to_xnT(r, gain, Bgain, final_inplace=False):
            xi = st_state["xtm"] % 2
            st_state["xtm"] += 1
            si = r
            xt = xn_tm[xi]
            if final_inplace:
                P.op("vector", lambda e: e.scalar_tensor_tensor(out=h32[:, r, :], in0=h32[:, r, :],
                                                                scalar=rstd[:, si:si + 1], in1=gain[:],
                                                                op0=ALU.mult, op1=ALU.mult),
                     reads=[Bssall, Bgain], writes=Bh[r])
                return
            P.op("vector", lambda e: e.scalar_tensor_tensor(out=xt[:], in0=h32[:, r, :],
                                                            scalar=rstd[:, si:si + 1], in1=gain[:],
                                                            op0=ALU.mult, op1=ALU.mult),
                 reads=Bh[r] + [Bssall, Bgain], writes=[Bxtm[xi]])
            for half in range(2):
                for j in range(8):
                    kc = half * 8 + j
                    P.op("tensor", lambda e, kc=kc, j=j: e.transpose(out=pst[:, j * 128:(j + 1) * 128],
                                                                     in_=xt[:, kc * 128:(kc + 1) * 128],
                                                                     identity=ident[:]),
                         reads=[Bxtm[xi], Bid], writes=[Bpst])
                eng = "scalar" if half == 0 else "vector"
                src = pst[:, :].rearrange("p (j n) -> p j n", n=128)
                dst = xnT[:, half * 8:(half + 1) * 8, r * 128:(r + 1) * 128]
                if eng == "scalar":
                    P.op("scalar", lambda e, src=src, dst=dst: e.activation(out=dst, in_=src, func=AF.Copy),
                         reads=[Bpst], writes=[BxnT[r]])
                else:
                    P.op("vector", lambda e, src=src, dst=dst: e.tensor_copy(out=dst, in_=src),
                         reads=[Bpst], writes=[BxnT[r]])

        for t in range(ntiles):
            t0 = t * T
            allBh = [b for r in range(RT) for b in Bh[r]]
            P.dma("sync", [lambda e, t0=t0: e.dma_start(
                out=h32[:], in_=h_in[t0:t0 + T, :].rearrange("(r p) d -> p r d", p=128))],
                Bh[0][0], writes=allBh)
            if has_mix:
                P.dma("sync", [lambda e, t0=t0: e.dma_start(
                    out=xnT[:], in_=mixT.rearrange("(kc p) n -> p kc n", p=128)[:, :, t0:t0 + T])],
                    BxnT[0], writes=BxnT)
                for db in range(4):
                    ri = next_ring()
                    wv = ring[ri][:, :].rearrange("p (kc n) -> p kc n", n=512)
                    P.dma("gpsimd", [lambda e, wv=wv, db=db: e.dma_start(
                        out=wv, in_=wo.rearrange("(kc p) n -> p kc n", p=128)[:, :, db * 512:(db + 1) * 512])],
                        Bring[ri], writes=[Bring[ri]])
                    for r in range(RT):
                        pi = next_acc()
                        for kc in range(KC):
                            P.op("tensor", lambda e, pi=pi, kc=kc, r=r, wv=wv: e.matmul(
                                psm[pi][:, :], lhsT=xnT[:, kc, r * 128:(r + 1) * 128], rhs=wv[:, kc, :],
                                start=(kc == 0), stop=(kc == KC - 1)),
                                reads=[BxnT[r], Bring[ri]], writes=[Bps[pi]])
                        P.op("vector", lambda e, pi=pi, r=r, db=db: e.tensor_tensor(
                            out=h32[:, r, db * 512:(db + 1) * 512], in0=psm[pi][:, :],
                            in1=h32[:, r, db * 512:(db + 1) * 512], op=ALU.add),
                            reads=[Bps[pi]], writes=[Bh[r][db]])
            rms_stats()
            for r in range(RT):
                rmsnorm_to_xnT(r, gf, Bgf)
            for g in range(HG if do_ffn else 0):
                hl0 = 0
                while hl0 < HCG:
                    n2 = min(2, HCG - hl0)
                    ri = next_ring()
                    wv = ring[ri][:, :2 * KC * n2 * 128].rearrange("p (a kc n) -> p a kc n", a=2, n=n2 * 128)
                    c0 = (g * HCG + hl0) * 128
                    P.dma("gpsimd", [
                        lambda e, wv=wv, c0=c0, n2=n2: e.dma_start(
                            out=wv[:, 0], in_=wg.rearrange("(kc p) n -> p kc n", p=128)[:, :, c0:c0 + n2 * 128]),
                        lambda e, wv=wv, c0=c0, n2=n2: e.dma_start(
                            out=wv[:, 1], in_=wu.rearrange("(kc p) n -> p kc n", p=128)[:, :, c0:c0 + n2 * 128]),
                    ], Bring[ri], writes=[Bring[ri]])
                    for h2 in range(n2):
                        hl = hl0 + h2
                        for tt in range(TT):
                            pa, pb = GU[st_state["gu"] % 2]
                            st_state["gu"] += 1
                            rd = [BxnT[r] for r in range(tt * 4, tt * 4 + 4)] + [Bring[ri]]
                            for a, pp in ((0, pa), (1, pb)):
                                for kc in range(KC):
                                    P.op("tensor", lambda e, a=a, pp=pp, kc=kc, wv=wv, h2=h2, tt=tt: e.matmul(
                                        psm[pp][:, :], lhsT=wv[:, a, kc, h2 * 128:(h2 + 1) * 128],
                                        rhs=xnT[:, kc, tt * 512:(tt + 1) * 512],
                                        start=(kc == 0), stop=(kc == KC - 1)),
                                        reads=rd, writes=[Bps[pp]])
                            si = st_state["gu"] % 2
                            P.op("scalar", lambda e, pa=pa, si=si: e.activation(
                                out=sg[si][:], in_=psm[pa][:, :], func=AF.Silu),
                                reads=[Bps[pa]], writes=[Bsg[si]])
                            P.op("vector", lambda e, pb=pb, si=si, hl=hl, tt=tt: e.tensor_tensor(
                                out=GT[:, hl, tt * 512:(tt + 1) * 512], in0=psm[pb][:, :], in1=sg[si][:],
                                op=ALU.mult),
                                reads=[Bps[pb], Bsg[si]], writes=[BGT[hl][tt]])
                    hl0 += n2
                DN = 512 if HCG * 512 <= 8192 else (256 if HCG * 256 <= 8192 else 128)
                for dp in range(D // DN):
                    ri = next_ring()
                    wv = ring[ri][:, :HCG * DN].rearrange("p (hc n) -> p hc n", n=DN)
                    P.dma("gpsimd", [lambda e, wv=wv, dp=dp, g=g, DN=DN: e.dma_start(
                        out=wv, in_=wd.rearrange("(hc p) n -> p hc n", p=128)[:, g * HCG:(g + 1) * HCG,
                                                                              dp * DN:(dp + 1) * DN])],
                        Bring[ri], writes=[Bring[ri]])
                    db = (dp * DN) // 512
                    for r in range(RT):
                        pi = next_acc()
                        for hl in range(HCG):
                            P.op("tensor", lambda e, pi=pi, hl=hl, r=r, wv=wv, DN=DN: e.matmul(
                                psm[pi][:, :DN], lhsT=GT[:, hl, r * 128:(r + 1) * 128], rhs=wv[:, hl, :],
                                start=(hl == 0), stop=(hl == HCG - 1)),
                                reads=[BGT[hl][r // 4], Bring[ri]], writes=[Bps[pi]])
                        P.op("vector", lambda e, pi=pi, r=r, dp=dp, DN=DN: e.tensor_tensor(
                            out=h32[:, r, dp * DN:(dp + 1) * DN], in0=psm[pi][:, :DN],
                            in1=h32[:, r, dp * DN:(dp + 1) * DN], op=ALU.add),
                            reads=[Bps[pi]], writes=[Bh[r][db]])
            if last:
                rms_stats()
                for r in range(RT):
                    rmsnorm_to_xnT(r, gn, Bgn, final_inplace=True)
                ev = P.dma("sync", [lambda e, t0=t0: e.dma_start(
                    out=h_out[t0:t0 + T, :].rearrange("(r p) d -> p r d", p=128), in_=h32[:])],
                    Bh[0][0], reads=allBh)
                P.finish([ev])
            else:
                ev = None
                if write_h:
                    ev = P.dma("sync", [lambda e, t0=t0: e.dma_start(
                        out=h_out[t0:t0 + T, :].rearrange("(r p) d -> p r d", p=128), in_=h32[:])],
                        Bh[0][0], reads=allBh)
                rms_stats()
                for r in range(RT):
                    rmsnorm_to_xnT(r, gn, Bgn)
                ev2 = P.dma("sync", [lambda e, t0=t0: e.dma_start(
                    out=xnT_out.rearrange("(kc p) n -> p kc n", p=128)[:, :, t0:t0 + T], in_=xnT[:])],
                    BxnT[0], reads=BxnT)
                P.finish([ev, ev2])
        P.emit()
    return nc


def build_Aodd(S=8192, NH=8):
    nc = bass.Bass("TRN2", target_bir_lowering=False)
    NB = S // 128
    NG = S // 512
    SCALE = 128 ** -0.5
    xnT_d = nc.dram_tensor("xnT", [D, S], BF16, kind="ExternalInput").ap()
    wq_d = nc.dram_tensor("wq", [D, NH * 128], F32, kind="ExternalInput").ap()
    wk_d = nc.dram_tensor("wk", [D, NH * 128], F32, kind="ExternalInput").ap()
    wvf_d = nc.dram_tensor("wvf", [D, NH, 129], F32, kind="ExternalInput").ap()
    bf_d = nc.dram_tensor("bf", [NH], F32, kind="ExternalInput").ap()
    c32_d = nc.dram_tensor("c32", [4, 128, 128], F32, kind="ExternalInput").ap()
    trix_d = nc.dram_tensor("trix", [128, 64], F32, kind="ExternalInput").ap()
    cbf_d = nc.dram_tensor("cbf", [2, 128, 128], BF16, kind="ExternalInput").ap()
    out_d = nc.dram_tensor("mixT", [NH * 128, S], BF16, kind="ExternalOutput").ap()

    P = Prog(nc)
    st = contextlib.ExitStack()
    with st:
        sb = lambda name, shape, dt: st.enter_context(nc.sbuf_tensor(name, shape, dt))
        xt = [sb("xt%d" % i, [128, KC, 512], BF16) for i in range(2)]
        wq = [sb("wq%d" % i, [128, KC, 128], BF16) for i in range(2)]
        wk = [sb("wk%d" % i, [128, KC, 128], BF16) for i in range(2)]
        wvf = [sb("wvf%d" % i, [128, KC, 130], BF16) for i in range(2)]
        QT = [sb("QT%d" % i, [128, S], BF16) for i in range(2)]
        KT = [sb("KT%d" % i, [128, S], BF16) for i in range(2)]
        V = [sb("V%d" % i, [128, NB, 130], BF16) for i in range(2)]
        Fp = [sb("Fp%d" % i, [128, 64], F32) for i in range(2)]
        lf = sb("lf", [128, 64], F32)
        lfT = sb("lfT", [64, 128], F32)
        Z = sb("Z", [128, 64], F32)
        cc = [sb("cc%d" % i, [128, 64], F32) for i in range(2)]
        cref = [sb("cref%d" % i, [128, 64], F32) for i in range(2)]
        boff = [sb("boff%d" % i, [128, 64], F32) for i in range(2)]
        bin_ = [sb("bin%d" % i, [128, 4, 4], F32) for i in range(2)]
        fi = [sb("fi%d" % i, [128, 4], F32) for i in range(2)]
        PToff = [sb("PToff%d" % i, [128, 512], BF16) for i in range(2)]
        PTin = [sb("PTin%d" % i, [128, 128], BF16) for i in range(2)]
        osb = [sb("osb%d" % i, [128, 130], F32) for i in range(2)]
        rc = [sb("rc%d" % i, [128, 1], F32) for i in range(2)]
        ao = [sb("ao%d" % i, [128, 128], BF16) for i in range(2)]
        mo = sb("mo", [128, S], BF16)
        c32 = sb("c32sb", [128, 4, 128], F32)
        trix = sb("trixsb", [128, 64], F32)
        cbf = sb("cbfsb", [128, 2, 128], BF16)
        bfb = sb("bfb", [128, NH], F32)
        psall = st.enter_context(nc.psum_tensor("psall", [128, 4096], F32))
        bank = lambda i: psall[:, i * 512:(i + 1) * 512]
        ps_off = [bank(0), bank(1)]
        acc_off = [bank(2)[:, 0:129], bank(2)[:, 256:385], bank(3)[:, 0:129], bank(3)[:, 256:385]]
        acc_in = [bank(4)[:, 0:129], bank(4)[:, 256:385], bank(5)[:, 0:129], bank(5)[:, 256:385]]
        ps_in = [bank(6)[:, 0:128], bank(7)[:, 0:128]]
        ps7 = [bank(6), bank(7)]
        U, ONES, E0, ID32 = (c32[:, k, :] for k in range(4))
        IDB, TRIM = cbf[:, 0, :], cbf[:, 1, :]

        Bxt = [Buf(), Buf()]
        Bw = [Buf(), Buf()]
        BQ = [[Buf() for _ in range(NG)] for _ in range(2)]
        BK = [[Buf() for _ in range(NG)] for _ in range(2)]
        BV = [[Buf() for _ in range(NG)] for _ in range(2)]
        BFp = [[Buf() for _ in range(NG)] for _ in range(2)]
        Bones = [Buf(), Buf()]
        Blf, BlfT, BZ = Buf(), Buf(), Buf()
        Bcc = [Buf(), Buf()]
        Bcref = [Buf(), Buf()]
        Bboff = [Buf(), Buf()]
        Bbin = [Buf(), Buf()]
        Bfi = [Buf(), Buf()]
        BPToff = [Buf(), Buf()]
        BPTin = [Buf() for _ in range(2)]
        Bosb = [Buf(), Buf()]
        Bao = [Buf(), Buf()]
        Bmo = [Buf() for _ in range(NB)]
        Bc32, Btrix, Bcbf, Bbfb = Buf(), Buf(), Buf(), Buf()
        Bps_off = [Buf(), Buf()]
        Bacc_off = [Buf(), Buf()]
        Bacc_in = [Buf(), Buf()]
        Bps_in = Buf()
        Bps7 = [Buf(), Buf()]
        Bps_in_s = Bps7
        ctr = dict(xt=0, p7=0, pin=0, fin=0)

        P.dma("sync", [lambda e: e.dma_start(out=c32[:], in_=c32_d.rearrange("k p n -> p k n"))], Bc32, writes=[Bc32])
        P.dma("sync", [lambda e: e.dma_start(out=trix[:], in_=trix_d)], Btrix, writes=[Btrix])
        P.dma("sync", [lambda e: e.dma_start(out=cbf[:], in_=cbf_d.rearrange("k p n -> p k n"))], Bcbf, writes=[Bcbf])
        P.dma("sync", [lambda e: e.dma_start(out=bfb[:], in_=bf_d.partition_broadcast(128))], Bbfb, writes=[Bbfb])
        for s in range(2):
            P.op("vector", lambda e, s=s: e.memset(V[s][:, :, 128:129], 1.0), writes=[Bones[s]])

        def load_w(h):
            ws = h % 2
            cs = slice(h * 128, (h + 1) * 128)
            r3 = lambda a: a.rearrange("(kc p) n -> p kc n", p=128)
            P.dma("gpsimd", [
                lambda e: e.dma_start(out=wq[ws][:], in_=r3(wq_d)[:, :, cs]),
                lambda e: e.dma_start(out=wk[ws][:], in_=r3(wk_d)[:, :, cs]),
                lambda e: e.dma_start(out=wvf[ws][:, :, 0:129], in_=wvf_d.rearrange("(kc p) h n -> p kc h n", p=128)[:, :, h, :]),
            ], Bw[ws], writes=[Bw[ws]])

        def inproj(h, t):
            s = h % 2
            ws = h % 2
            xi = ctr["xt"] % 2
            ctr["xt"] += 1
            xb = xt[xi]
            P.dma("sync", [lambda e: e.dma_start(
                out=xb[:], in_=xnT_d.rearrange("(kc p) n -> p kc n", p=128)[:, :, t * 512:(t + 1) * 512])],
                Bxt[xi], writes=[Bxt[xi]])
            for (w_, dst, Bd) in ((wq, QT, BQ), (wk, KT, BK)):
                pi = ctr["p7"] % 2
                ctr["p7"] += 1
                for kc in range(KC):
                    P.op("tensor", lambda e, pi=pi, kc=kc, w_=w_: e.matmul(
                        ps7[pi], lhsT=w_[ws][:, kc, :], rhs=xb[:, kc, :],
                        start=(kc == 0), stop=(kc == KC - 1)),
                        reads=[Bw[ws], Bxt[xi]], writes=[Bps7[pi]])
                c0 = t * 512
                P.op("vector", lambda e, pi=pi, dst=dst, c0=c0: e.tensor_copy(out=dst[s][:, c0:c0 + 512], in_=ps7[pi]),
                     reads=[Bps7[pi]], writes=[Bd[s][t]])
            for r in range(4):
                pi = ctr["p7"] % 2
                ctr["p7"] += 1
                blk = t * 4 + r
                for kc in range(KC):
                    P.op("tensor", lambda e, pi=pi, kc=kc, r=r: e.matmul(
                        ps7[pi][:, 0:129], lhsT=xb[:, kc, r * 128:(r + 1) * 128], rhs=wvf[ws][:, kc, 0:129],
                        start=(kc == 0), stop=(kc == KC - 1)),
                        reads=[Bw[ws], Bxt[xi]], writes=[Bps7[pi]])
                P.op("vector", lambda e, pi=pi, blk=blk: e.tensor_copy(out=V[s][:, blk, 0:128], in_=ps7[pi][:, 0:128]),
                     reads=[Bps7[pi]], writes=[BV[s][t]])
                P.op("vector", lambda e, pi=pi, blk=blk: e.tensor_copy(out=Fp[s][:, blk:blk + 1], in_=ps7[pi][:, 128:129]),
                     reads=[Bps7[pi]], writes=[BFp[s][t]])

        def gates(h):
            s = h % 2
            allF = BFp[s]
            P.op("scalar", lambda e: e.activation(out=lf[:, 0:NB], in_=Fp[s][:, 0:NB], func=AF.Sigmoid,
                                                  bias=bfb[:, h:h + 1], scale=1.0),
                 reads=allF + [Bbfb], writes=[Blf])
            P.op("scalar", lambda e: e.activation(out=lf[:, 0:NB], in_=lf[:, 0:NB], func=AF.Ln), writes=[Blf])
            pa, pb = 0, 1
            P.op("tensor", lambda e: e.transpose(out=ps7[pa][0:NB, 0:128], in_=lf[:, 0:NB], identity=ID32),
                 reads=[Blf, Bc32], writes=[Bps7[pa]])
            P.op("vector", lambda e: e.tensor_copy(out=lfT[0:NB, :], in_=ps7[pa][0:NB, 0:128]),
                 reads=[Bps7[pa]], writes=[BlfT])
            P.op("tensor", lambda e: e.matmul(ps7[pb][:, 0:NB], lhsT=lfT[0:NB, :], rhs=trix[0:NB, 0:NB],
                                              start=True, stop=True),
                 reads=[BlfT, Btrix], writes=[Bps7[pb]])
            P.op("vector", lambda e: e.tensor_copy(out=Z[:, 0:NB], in_=ps7[pb][:, 0:NB]),
                 reads=[Bps7[pb]], writes=[BZ])
            P.op("tensor", lambda e: e.matmul(ps7[pa][:, 0:NB], lhsT=U, rhs=lf[:, 0:NB], start=True, stop=False),
                 reads=[Blf, Bc32], writes=[Bps7[pa]])
            P.op("tensor", lambda e: e.matmul(ps7[pa][:, 0:NB], lhsT=ONES, rhs=Z[:, 0:NB], start=False, stop=True),
                 reads=[BZ, Bc32], writes=[Bps7[pa]])
            P.op("vector", lambda e: e.tensor_copy(out=cc[s][:, 0:NB], in_=ps7[pa][:, 0:NB]),
                 reads=[Bps7[pa]], writes=[Bcc[s]])
            P.op("tensor", lambda e: e.matmul(ps7[pb][:, 0:NB], lhsT=E0, rhs=cc[s][:, 0:NB], start=True, stop=True),
                 reads=[Bcc[s], Bc32], writes=[Bps7[pb]])
            P.op("vector", lambda e: e.tensor_copy(out=cref[s][:, 0:NB], in_=ps7[pb][:, 0:NB]),
                 reads=[Bps7[pb]], writes=[Bcref[s]])

        def attention(h, G):
            s = h % 2
            gb = G % 2
            i0 = 4 * G
            if G > 0:
                P.op("vector", lambda e: e.tensor_scalar(out=boff[gb][:, 0:i0], in0=cc[s][:, 0:i0], scalar1=-1.0,
                                                         scalar2=cref[s][:, i0:i0 + 1], op0=ALU.mult, op1=ALU.add),
                     reads=[Bcc[s], Bcref[s]], writes=[Bboff[gb]])
                P.op("vector", lambda e: e.tensor_scalar(out=fi[gb][:, 0:4], in0=cref[s][:, i0:i0 + 4],
                                                         scalar1=cref[s][:, i0:i0 + 1], scalar2=0.0, op0=ALU.subtract, op1=ALU.add),
                     reads=[Bcref[s]], writes=[Bfi[gb]])
                P.op("scalar", lambda e: e.activation(out=fi[gb][:, 0:4], in_=fi[gb][:, 0:4], func=AF.Exp),
                     writes=[Bfi[gb]])
            for ii in range(4):
                P.op("vector", lambda e, ii=ii: e.tensor_scalar(
                    out=bin_[gb][:, ii, 0:ii + 1], in0=cc[s][:, i0:i0 + ii + 1], scalar1=-1.0,
                    scalar2=cref[s][:, i0 + ii:i0 + ii + 1], op0=ALU.mult, op1=ALU.add),
                    reads=[Bcc[s], Bcref[s]], writes=[Bbin[gb]])
            for j in range(i0):
                sbi = j % 2
                P.op("tensor", lambda e, j=j, sbi=sbi: e.matmul(
                    ps_off[sbi], lhsT=KT[s][:, j * 128:(j + 1) * 128], rhs=QT[s][:, i0 * 128:(i0 + 4) * 128],
                    start=True, stop=True),
                    reads=[BK[s][j // 4], BQ[s][G]], writes=[Bps_off[sbi]])
                P.op("scalar", lambda e, j=j, sbi=sbi: e.activation(
                    out=PToff[sbi][:], in_=ps_off[sbi], func=AF.Exp, bias=boff[gb][:, j:j + 1], scale=SCALE),
                    reads=[Bps_off[sbi], Bboff[gb]], writes=[BPToff[sbi]])
                for ii in range(4):
                    P.op("tensor", lambda e, j=j, sbi=sbi, ii=ii: e.matmul(
                        acc_off[ii], lhsT=PToff[sbi][:, ii * 128:(ii + 1) * 128], rhs=V[s][:, j, 0:129],
                        start=(j == 0 and ii % 2 == 0), stop=(j == i0 - 1), skip_group_check=True),
                        reads=[BPToff[sbi], BV[s][j // 4], Bones[s]], writes=[Bacc_off[ii // 2]])
            for ii in range(4):
                i = i0 + ii
                for jj in range(ii + 1):
                    j = i0 + jj
                    sl = ctr["p7"] % 2
                    ctr["p7"] += 1
                    P.op("tensor", lambda e, i=i, j=j, sl=sl: e.matmul(
                        ps_in[sl], lhsT=KT[s][:, j * 128:(j + 1) * 128], rhs=QT[s][:, i * 128:(i + 1) * 128],
                        start=True, stop=True, skip_group_check=True),
                        reads=[BK[s][G], BQ[s][G]], writes=[Bps_in_s[sl]])
                    P.op("scalar", lambda e, ii=ii, jj=jj, sl=sl: e.activation(
                        out=PTin[sl][:], in_=ps_in[sl], func=AF.Exp, bias=bin_[gb][:, ii, jj:jj + 1], scale=SCALE),
                        reads=[Bps_in_s[sl], Bbin[gb]], writes=[BPTin[sl]])
                    if jj == ii:
                        P.op("vector", lambda e, sl=sl: e.tensor_tensor(out=PTin[sl][:], in0=PTin[sl][:], in1=TRIM,
                                                                        op=ALU.mult),
                             reads=[Bcbf], writes=[BPTin[sl]])
                    P.op("tensor", lambda e, ii=ii, jj=jj, j=j, sl=sl: e.matmul(
                        acc_in[ii], lhsT=PTin[sl][:], rhs=V[s][:, j, 0:129],
                        start=(jj == 0 and ii % 2 == 0), stop=(jj == ii), skip_group_check=True),
                        reads=[BPTin[sl], BV[s][G], Bones[s]], writes=[Bacc_in[ii // 2]])
            for ii in range(4):
                i = i0 + ii
                fb = ctr["fin"] % 2
                ctr["fin"] += 1
                P.op("vector", lambda e, ii=ii, fb=fb: e.tensor_copy(out=osb[fb][:, 0:129], in_=acc_in[ii]),
                     reads=[Bacc_in[ii // 2]], writes=[Bosb[fb]])
                if G > 0:
                    P.op("vector", lambda e, ii=ii, fb=fb: e.scalar_tensor_tensor(
                        out=osb[fb][:, 0:129], in0=acc_off[ii], scalar=fi[gb][:, ii:ii + 1], in1=osb[fb][:, 0:129],
                        op0=ALU.mult, op1=ALU.add),
                        reads=[Bacc_off[ii // 2], Bfi[gb]], writes=[Bosb[fb]])
                P.op("vector", lambda e, fb=fb: e.reciprocal(out=rc[fb][:], in_=osb[fb][:, 128:129]),
                     reads=[Bosb[fb]], writes=[Bao[fb]])
                P.op("vector", lambda e, fb=fb: e.tensor_scalar(out=ao[fb][:], in0=osb[fb][:, 0:128], scalar1=rc[fb][:, 0:1],
                                                                scalar2=0.0, op0=ALU.mult, op1=ALU.add),
                     reads=[Bosb[fb]], writes=[Bao[fb]])
                sl = ctr["p7"] % 2
                ctr["p7"] += 1
                pt = ps_in[sl].bitcast(BF16)[:, 0:128]
                P.op("tensor", lambda e, fb=fb, pt=pt: e.transpose(out=pt, in_=ao[fb][:], identity=IDB),
                     reads=[Bao[fb], Bcbf], writes=[Bps_in_s[sl]])
                P.op("vector", lambda e, i=i, pt=pt: e.tensor_copy(out=mo[:, i * 128:(i + 1) * 128], in_=pt),
                     reads=[Bps_in_s[sl]], writes=[Bmo[i]])

        load_w(0)
        for t in range(NG):
            inproj(0, t)
        gates(0)
        if NH > 1:
            load_w(1)
        evs = []
        for h in range(NH):
            for G in range(NG):
                attention(h, G)
                if h + 1 < NH:
                    inproj(h + 1, G)
            if h + 1 < NH:
                gates(h + 1)
            if h + 2 < NH:
                load_w(h + 2)
            evs.append(P.dma("sync", [lambda e, h=h: e.dma_start(out=out_d[h * 128:(h + 1) * 128, :], in_=mo[:])],
                             Bmo[0], reads=Bmo))
        P.finish(evs)
        P.emit()
    return nc


def consts_Aodd():
    import ml_dtypes
    U = np.triu(np.ones((128, 128), np.float32))
    ones = np.ones((128, 128), np.float32)
    E0 = np.zeros((128, 128), np.float32); E0[0, :] = 1
    I = np.eye(128, dtype=np.float32)
    c32 = np.stack([U, ones, E0, I])
    trix = np.zeros((128, 64), np.float32)
    trix[:64, :] = np.triu(np.ones((64, 64), np.float32), 1)
    trim = np.triu(np.ones((128, 128), np.float32))
    cbf = np.stack([I, trim]).astype(ml_dtypes.bfloat16)
    return dict(c32=c32, trix=trix, cbf=cbf)


STAGE = 9

NFM = 1152
NTM = 1412


def build_Aeven(S=8192):
    nc = bass.Bass("TRN2", target_bir_lowering=False)
    NB = S // 128
    NT = S // 512
    xnT_d = nc.dram_tensor("xnT", [D, S], BF16, kind="ExternalInput").ap()
    wfm_d = nc.dram_tensor("wfm", [D, NFM], F32, kind="ExternalInput").ap()
    wtm_d = nc.dram_tensor("wtm", [D, NTM], F32, kind="ExternalInput").ap()
    vec_d = nc.dram_tensor("vec", [12 + 512], F32, kind="ExternalInput").ap()
    tab_d = nc.dram_tensor("tab", [128, 4, 512], F32, kind="ExternalInput").ap()
    c32_d = nc.dram_tensor("c32", [3, 128, 128], F32, kind="ExternalInput").ap()
    cbf_d = nc.dram_tensor("cbf", [2, 128, 128], BF16, kind="ExternalInput").ap()
    out_d = nc.dram_tensor("mixT", [1024, S], BF16, kind="ExternalOutput").ap()

    P = Prog(nc)
    st = contextlib.ExitStack()
    with st:
        sb = lambda name, shape, dt: st.enter_context(nc.sbuf_tensor(name, shape, dt))
        wfm = sb("wfm_sb", [128, KC, NFM], BF16)
        wtm = sb("wtm_sb", [128, KC, NTM], BF16)
        xt = [sb("xt%d" % i, [128, KC, 512], BF16) for i in range(2)]
        QsT = sb("QsT", [64, 8, 512], BF16)
        KsT = sb("KsT", [64, 2, 1024], BF16)
        Vs = sb("Vs", [128, 8, 2, 66], BF16)
        qmT = sb("qmT", [128, 2, 512], BF16)
        kmT = sb("kmT", [128, 2, 512], BF16)
        km = sb("km", [128, 4, 2, 128], BF16)
        vm = sb("vm", [128, 4, 2, 258], BF16)
        osig = sb("osig", [128, 4, 2, 256], BF16)
        Gp = sb("Gp", [128, NB, 4], F32)
        vec = sb("vec_sb", [128, 12 + 512], F32)
        tab = sb("tab_sb", [128, 4, 512], F32)
        c32 = sb("c32_sb", [128, 3, 128], F32)
        cbf = sb("cbf_sb", [128, 2, 128], BF16)
        esink = sb("esink", [128, 8], F32)
        b15 = sb("b15", [128, 4], F32)
        gt = {k: sb("g_" + k, [128, NB, 2], F32) for k in
              ("li", "lf", "b", "bl", "a", "am", "w", "wsc", "dec", "scl", "ints", "bnd", "t1")}
        aT = sb("aT", [128, 128], F32)
        amT = sb("amT", [128, 1], F32)
        dg = sb("dg", [128, 128], F32)
        mall = sb("mall", [128, NB + 1, 2], F32)
        tmpf = [sb("tmpf%d" % i, [128, 512], F32) for i in range(2)]
        PTs = [sb("PTs%d" % i, [128, 512], BF16) for i in range(2)]
        att = [sb("att%d" % i, [128, 4, 64], BF16) for i in range(2)]
        dn = [sb("dn%d" % i, [128, 4], F32) for i in range(2)]
        qkT = [sb("qkT%d" % i, [128, 128], BF16) for i in range(2)]
        kw = [sb("kw%d" % i, [128, 128], BF16) for i in range(2)]
        Cst = [sb("Cst%d" % i, [128, 257], F32) for i in range(2)]
        Ct = [sb("Ct%d" % i, [128, 258], BF16) for i in range(2)]
        tC = [sb("tC%d" % i, [128, 257], F32) for i in range(2)]
        hh = [sb("hh%d" % i, [128, 256], F32) for i in range(2)]
        hj = sb("hj", [128, 256], BF16)
        dd = [sb("dd%d" % i, [128, 2], F32) for i in range(2)]
        ssq = sb("ssq", [128, 8], F32)
        rstd = sb("rstd", [128, 8], F32)
        epsc = sb("epsc", [128, 1], F32)
        hn = [sb("hn%d" % i, [128, 256], F32) for i in range(2)]
        mem = [sb("mem%d" % i, [128, 256], BF16) for i in range(2)]
        mo = [sb("mo%d" % i, [128, 8, 512], BF16) for i in range(2)]
        psall = st.enter_context(nc.psum_tensor("psall", [128, 4096], F32))
        bank = [psall[:, i * 512:(i + 1) * 512] for i in range(8)]
        Bbank = [Buf() for _ in range(8)]
        U, ONES, ID32 = (c32[:, k, :] for k in range(3))
        IDB, CAUS = cbf[:, 0, :], cbf[:, 1, :]
        SINK = lambda j: vec[:, j:j + 1]
        GAIN = lambda m: vec[:, 12 + m * 256:12 + (m + 1) * 256]

        Bw, Bvec, Btab, Bc32, Bcbf, Bes, Bb15, Beps = (Buf() for _ in range(8))
        Bxt = [Buf(), Buf()]
        BQs, Bqm, Bkm, Bkmt, Bvm, Bos = (Buf() for _ in range(6))
        BKs = [Buf() for _ in range(2)]
        BVs = [Buf() for _ in range(2)]
        BVones = Buf()
        BGp = [Buf() for _ in range(NT)]
        Bg = {k: Buf() for k in gt}
        BaT, BamT, Bdg, Bmall = Buf(), Buf(), Buf(), Buf()
        Btmpf, BPTs, Batt, Bdn, BqkT, Bkw = ([Buf(), Buf()] for _ in range(6))
        BC, BCt, BtC, Bhh, Bdd, Bhn, Bmem, Bmo = ([Buf(), Buf()] for _ in range(8))
        Bhj, Bssq = Buf(), Buf()
        ctr = dict(xt=0, pm=0, sc=0, k2=0, hh=0)
        MISC = [0, 1, 2]
        SC = [3, 4]
        ACC = [5, 6, 7]

        def nb_(lst, key):
            i = lst[ctr.setdefault(key, 0) % len(lst)]
            ctr[key] += 1
            return i

        P.dma("sync", [lambda e: e.dma_start(out=vec[:], in_=vec_d.partition_broadcast(128))], Bvec, writes=[Bvec])
        P.dma("sync", [lambda e: e.dma_start(out=tab[:], in_=tab_d)], Btab, writes=[Btab])
        P.dma("sync", [lambda e: e.dma_start(out=c32[:], in_=c32_d.rearrange("k p n -> p k n"))], Bc32, writes=[Bc32])
        P.dma("sync", [lambda e: e.dma_start(out=cbf[:], in_=cbf_d.rearrange("k p n -> p k n"))], Bcbf, writes=[Bcbf])
        r3 = lambda a: a.rearrange("(kc p) n -> p kc n", p=128)
        fns = []
        for k0 in range(0, KC, 4):
            fns.append(lambda e, k0=k0: e.dma_start(out=wtm[:, k0:k0 + 4, :], in_=r3(wtm_d)[:, k0:k0 + 4, :]))
        for k0 in range(0, KC, 4):
            fns.append(lambda e, k0=k0: e.dma_start(out=wfm[:, k0:k0 + 4, :], in_=r3(wfm_d)[:, k0:k0 + 4, :]))
        P.dma("gpsimd", fns, Bw, writes=[Bw])
        P.op("vector", lambda e: e.memset(epsc[:], EPS), writes=[Beps])
        P.op("vector", lambda e: e.memset(Vs[:, :, :, 64:65], 1.0), writes=[BVones])
        P.op("vector", lambda e: e.memset(vm[:, :, :, 256:257], 1.0), writes=[Bvm])
        for m in range(2):
            P.op("vector", lambda e, m=m: e.memset(Cst[m][:], 0.0), writes=[BC[m]])
            P.op("vector", lambda e, m=m: e.memset(Ct[m][:], 0.0), writes=[BCt[m]])
        P.op("scalar", lambda e: e.activation(out=esink[:], in_=vec[:, 0:8], func=AF.Exp), reads=[Bvec], writes=[Bes])
        P.op("vector", lambda e: e.tensor_scalar(out=b15[:], in0=vec[:, 8:12], scalar1=1.0 / 15.0, scalar2=0.0,
                                                 op0=ALU.mult, op1=ALU.add), reads=[Bvec], writes=[Bb15])

        def load_x(t):
            xi = ctr["xt"] % 2
            ctr["xt"] += 1
            P.dma("sync", [lambda e: e.dma_start(out=xt[xi][:], in_=r3(xnT_d)[:, :, t * 512:(t + 1) * 512])],
                  Bxt[xi], writes=[Bxt[xi]])
            return xi

        for t in range(NT):
            xi = load_x(t)
            pb = nb_(MISC, "pm")
            for r in range(4):
                for kc in range(KC):
                    P.op("tensor", lambda e, pb=pb, r=r, kc=kc, xi=xi: e.matmul(
                        bank[pb][:, r * 4:r * 4 + 4], lhsT=xt[xi][:, kc, r * 128:(r + 1) * 128],
                        rhs=wtm[:, kc, 384:388], start=(kc == 0 and r == 0), stop=(kc == KC - 1),
                        skip_group_check=True),
                        reads=[Bw, Bxt[xi]], writes=[Bbank[pb]])
            P.op("vector", lambda e, pb=pb, t=t: e.tensor_copy(
                out=Gp[:, t * 4:(t + 1) * 4, :], in_=bank[pb][:, 0:16].rearrange("p (r c) -> p r c", c=4)),
                reads=[Bbank[pb]], writes=[BGp[t]])

        GB_ON = STAGE >= 2
        NBC = NB * 2
        flat = lambda k: gt[k][:, :, :].rearrange("p c m -> p (c m)")
        li, lf, b_, bl, a_, am, w_, wsc, dec, scl, ints, bnd, t1 = (gt[k] for k in
                                                                  ("li", "lf", "b", "bl", "a", "am", "w", "wsc", "dec",
                                                                   "scl", "ints", "bnd", "t1"))
        for m in range(2):
            P.op("scalar", lambda e, m=m: e.activation(out=li[:, :, m], in_=Gp[:, :, m], func=AF.Tanh,
                                                       bias=b15[:, m:m + 1], scale=1.0 / 15.0),
                 reads=BGp + [Bb15], writes=[Bg["li"]])
            P.op("scalar", lambda e, m=m: e.activation(out=lf[:, :, m], in_=Gp[:, :, 2 + m], func=AF.Tanh,
                                                       bias=b15[:, 2 + m:3 + m], scale=1.0 / 15.0),
                 reads=BGp + [Bb15], writes=[Bg["lf"]])
        P.op("vector", lambda e: e.tensor_scalar(out=flat("li"), in0=flat("li"), scalar1=15.0, scalar2=0.0,
                                                 op0=ALU.mult, op1=ALU.add), writes=[Bg["li"]])
        P.op("scalar", lambda e: e.activation(out=flat("lf"), in_=flat("lf"), func=AF.Sigmoid, scale=15.0),
             writes=[Bg["lf"]])
        P.op("scalar", lambda e: e.activation(out=flat("lf"), in_=flat("lf"), func=AF.Ln), writes=[Bg["lf"]])
        pb = nb_(MISC, "pm")
        P.op("tensor", lambda e, pb=pb: e.matmul(bank[pb][:, 0:NBC], lhsT=U, rhs=flat("lf"), start=True, stop=True),
             reads=[Bg["lf"], Bc32], writes=[Bbank[pb]])
        P.op("vector", lambda e, pb=pb: e.tensor_copy(out=flat("b"), in_=bank[pb][:, 0:NBC]), reads=[Bbank[pb]],
             writes=[Bg["b"]])
        pb2 = nb_(MISC, "pm")
        P.op("tensor", lambda e, pb2=pb2: e.matmul(bank[pb2][:, 0:NBC], lhsT=ONES, rhs=flat("lf"), start=True, stop=True),
             reads=[Bg["lf"], Bc32], writes=[Bbank[pb2]])
        P.op("vector", lambda e, pb2=pb2: e.tensor_copy(out=flat("bl"), in_=bank[pb2][:, 0:NBC]), reads=[Bbank[pb2]],
             writes=[Bg["bl"]])
        P.op("vector", lambda e: e.tensor_tensor(out=flat("a"), in0=flat("bl"), in1=flat("b"), op=ALU.subtract),
             reads=[Bg["bl"], Bg["b"]], writes=[Bg["a"]])
        P.op("vector", lambda e: e.tensor_tensor(out=flat("a"), in0=flat("a"), in1=flat("li"), op=ALU.add),
             reads=[Bg["li"]], writes=[Bg["a"]])
        for c0 in range(0, NBC, 128):
            ncol = min(128, NBC - c0)
            pb = nb_(MISC, "pm")
            P.op("tensor", lambda e, pb=pb, c0=c0, ncol=ncol: e.transpose(
                out=bank[pb][0:ncol, 0:128], in_=flat("a")[:, c0:c0 + ncol], identity=ID32),
                reads=[Bg["a"], Bc32], writes=[Bbank[pb]])
            P.op("vector", lambda e, pb=pb, ncol=ncol: e.tensor_reduce(
                out=amT[0:ncol, :], in_=bank[pb][0:ncol, 0:128], axis=AX.X, op=ALU.max),
                reads=[Bbank[pb]], writes=[BamT])
            P.op("vector", lambda e, ncol=ncol: e.tensor_scalar(
                out=dg[0:ncol, 0:ncol], in0=ID32[0:ncol, 0:ncol], scalar1=amT[0:ncol, 0:1], scalar2=0.0,
                op0=ALU.mult, op1=ALU.add), reads=[BamT, Bc32], writes=[Bdg])
            pb = nb_(MISC, "pm")
            P.op("tensor", lambda e, pb=pb, ncol=ncol: e.matmul(
                bank[pb][:, 0:ncol], lhsT=ONES[0:ncol, :], rhs=dg[0:ncol, 0:ncol], start=True, stop=True),
                reads=[Bdg, Bc32], writes=[Bbank[pb]])
            P.op("vector", lambda e, pb=pb, c0=c0, ncol=ncol: e.tensor_copy(
                out=flat("am")[:, c0:c0 + ncol], in_=bank[pb][:, 0:ncol]), reads=[Bbank[pb]], writes=[Bg["am"]])
        P.op("vector", lambda e: e.tensor_tensor(out=flat("t1"), in0=flat("a"), in1=flat("am"), op=ALU.subtract),
             reads=[Bg["a"], Bg["am"]], writes=[Bg["t1"]])
        P.op("scalar", lambda e: e.activation(out=flat("w"), in_=flat("t1"), func=AF.Exp), reads=[Bg["t1"]],
             writes=[Bg["w"]])
        P.op("vector", lambda e: e.tensor_scalar(out=flat("wsc"), in0=flat("w"), scalar1=128 ** -0.5, scalar2=0.0,
                                                 op0=ALU.mult, op1=ALU.add), reads=[Bg["w"]], writes=[Bg["wsc"]])
        P.op("vector", lambda e: e.memset(mall[:, 0, :], 0.0), writes=[Bmall])
        for c in range(NB):
            P.op("vector", lambda e, c=c: e.tensor_tensor(out=mall[:, c + 1, :], in0=mall[:, c, :], in1=bl[:, c, :],
                                                          op=ALU.add), reads=[Bg["bl"]], writes=[Bmall])
            P.op("vector", lambda e, c=c: e.tensor_tensor(out=mall[:, c + 1, :], in0=mall[:, c + 1, :],
                                                          in1=am[:, c, :], op=ALU.max), reads=[Bg["am"]],
                 writes=[Bmall])
        mprev = mall[:, 0:NB, :].rearrange("p c m -> p (c m)")
        mnext = mall[:, 1:NB + 1, :].rearrange("p c m -> p (c m)")
        P.op("vector", lambda e: e.tensor_tensor(out=flat("t1"), in0=flat("bl"), in1=mprev, op=ALU.add),
             reads=[Bg["bl"], Bmall], writes=[Bg["t1"]])
        P.op("vector", lambda e: e.tensor_tensor(out=flat("dec"), in0=flat("t1"), in1=mnext, op=ALU.subtract),
             reads=[Bg["t1"], Bmall], writes=[Bg["dec"]])
        P.op("scalar", lambda e: e.activation(out=flat("dec"), in_=flat("dec"), func=AF.Exp), writes=[Bg["dec"]])
        P.op("vector", lambda e: e.tensor_tensor(out=flat("scl"), in0=flat("am"), in1=mnext, op=ALU.subtract),
             reads=[Bg["am"], Bmall], writes=[Bg["scl"]])
        P.op("scalar", lambda e: e.activation(out=flat("scl"), in_=flat("scl"), func=AF.Exp), writes=[Bg["scl"]])
        P.op("vector", lambda e: e.tensor_tensor(out=flat("ints"), in0=flat("t1"), in1=flat("am"), op=ALU.subtract),
             reads=[Bg["t1"], Bg["am"]], writes=[Bg["ints"]])
        P.op("scalar", lambda e: e.activation(out=flat("ints"), in_=flat("ints"), func=AF.Exp), writes=[Bg["ints"]])
        P.op("vector", lambda e: e.tensor_scalar(out=flat("ints"), in0=flat("ints"), scalar1=128 ** -0.5, scalar2=0.0,
                                                 op0=ALU.mult, op1=ALU.add), writes=[Bg["ints"]])
        P.op("vector", lambda e: e.tensor_tensor(out=flat("bnd"), in0=flat("bl"), in1=flat("am"), op=ALU.subtract),
             reads=[Bg["bl"], Bg["am"]], writes=[Bg["bnd"]])
        P.op("vector", lambda e: e.tensor_tensor(out=flat("bnd"), in0=flat("bnd"), in1=flat("b"), op=ALU.subtract),
             reads=[Bg["b"]], writes=[Bg["bnd"]])
        P.op("scalar", lambda e: e.activation(out=flat("bnd"), in_=flat("bnd"), func=AF.Exp), writes=[Bg["bnd"]])
        gate_reads = [Bg[k] for k in ("wsc", "w", "dec", "scl", "ints", "bnd")]

        evs = []
        def tile(t):
            xi = load_x(t)
            xb = xt[xi]
            mi = t % 2
            if STAGE < 3:
                return
            def fm_group(c0, M, dst, Bd):
                pb = nb_(MISC, "pm")
                for kc in range(KC):
                    P.op("tensor", lambda e, pb=pb, kc=kc: e.matmul(
                        bank[pb][0:M, :], lhsT=wfm[:, kc, c0:c0 + M], rhs=xb[:, kc, :],
                        start=(kc == 0), stop=(kc == KC - 1)),
                        reads=[Bw, Bxt[xi]], writes=[Bbank[pb]])
                P.op("vector", lambda e, pb=pb: e.tensor_copy(out=dst, in_=bank[pb][0:M, :]),
                     reads=[Bbank[pb]], writes=[Bd])
            for hq in range(8):
                fm_group(hq * 64, 64, QsT[:, hq, :], BQs)
            for kv in range(2):
                fm_group(512 + kv * 64, 64, KsT[:, kv, (t % 2) * 512:(t % 2 + 1) * 512], BKs[t % 2])
            for m in range(2):
                fm_group(640 + m * 128, 128, qmT[:, m, :], Bqm)
                fm_group(896 + m * 128, 128, kmT[:, m, :], Bkmt)
            for r in range(4):
                blk = t * 4 + r
                pb = nb_(MISC, "pm")
                for kc in range(KC):
                    P.op("tensor", lambda e, pb=pb, kc=kc, r=r: e.matmul(
                        bank[pb][:, 0:384], lhsT=xb[:, kc, r * 128:(r + 1) * 128], rhs=wtm[:, kc, 0:384],
                        start=(kc == 0), stop=(kc == KC - 1)),
                        reads=[Bw, Bxt[xi]], writes=[Bbank[pb]])
                P.op("vector", lambda e, pb=pb, blk=blk: e.tensor_copy(
                    out=Vs[:, blk % 8, :, 0:64], in_=bank[pb][:, 0:128].rearrange("p (k d) -> p k d", d=64)),
                    reads=[Bbank[pb]], writes=[BVs[t % 2]])
                P.op("vector", lambda e, pb=pb, r=r: e.tensor_copy(
                    out=km[:, r, :, :], in_=bank[pb][:, 128:384].rearrange("p (m d) -> p m d", d=128)),
                    reads=[Bbank[pb]], writes=[Bkm])
                pb = nb_(MISC, "pm")
                for kc in range(KC):
                    P.op("tensor", lambda e, pb=pb, kc=kc, r=r: e.matmul(
                        bank[pb][:, :], lhsT=xb[:, kc, r * 128:(r + 1) * 128], rhs=wtm[:, kc, 388:900],
                        start=(kc == 0), stop=(kc == KC - 1)),
                        reads=[Bw, Bxt[xi]], writes=[Bbank[pb]])
                P.op("vector", lambda e, pb=pb, r=r: e.tensor_copy(
                    out=vm[:, r, :, 0:256], in_=bank[pb][:, :].rearrange("p (m d) -> p m d", d=256)),
                    reads=[Bbank[pb]], writes=[Bvm])
                pb = nb_(MISC, "pm")
                for kc in range(KC):
                    P.op("tensor", lambda e, pb=pb, kc=kc, r=r: e.matmul(
                        bank[pb][:, :], lhsT=xb[:, kc, r * 128:(r + 1) * 128], rhs=wtm[:, kc, 900:1412],
                        start=(kc == 0), stop=(kc == KC - 1)),
                        reads=[Bw, Bxt[xi]], writes=[Bbank[pb]])
                P.op("scalar", lambda e, pb=pb, r=r: e.activation(
                    out=osig[:, r, :, :], in_=bank[pb][:, :].rearrange("p (m d) -> p m d", d=256), func=AF.Sigmoid),
                    reads=[Bbank[pb]], writes=[Bos])
            for r in range(4 if STAGE >= 4 else 0):
                n = t * 4 + r
                for kv in range(2):
                    pa = nb_(ACC, "acc")
                    accv = bank[pa][:, 0:260].rearrange("p (g d) -> p g d", d=65)
                    kbs = [(0, n - 1), (1, n)] if n > 0 else [(1, n)]
                    for bi, (kb, kblk) in enumerate(kbs):
                        ps = nb_(SC, "sc")
                        P.op("tensor", lambda e, ps=ps, kv=kv, kblk=kblk, r=r: e.matmul(
                            bank[ps][:, :], lhsT=KsT[:, kv, (kblk % 8) * 128:(kblk % 8 + 1) * 128],
                            rhs=QsT[:, kv * 4:kv * 4 + 4, r * 128:(r + 1) * 128], start=True, stop=True),
                            reads=[BKs[(kblk // 4) % 2], BQs], writes=[Bbank[ps]])
                        k2 = ctr["k2"] % 2
                        ctr["k2"] += 1
                        P.op("vector", lambda e, ps=ps, k2=k2, kb=kb, kv=kv: e.scalar_tensor_tensor(
                            out=tmpf[k2][:], in0=bank[ps][:, :], scalar=0.125, in1=tab[:, kb * 2 + kv, :],
                            op0=ALU.mult, op1=ALU.add),
                            reads=[Bbank[ps], Btab], writes=[Btmpf[k2]])
                        P.op("scalar", lambda e, k2=k2: e.activation(out=PTs[k2][:], in_=tmpf[k2][:], func=AF.Exp),
                             reads=[Btmpf[k2]], writes=[BPTs[k2]])
                        for g in range(4):
                            P.op("tensor", lambda e, k2=k2, g=g, kblk=kblk, kv=kv, bi=bi, accv=accv, nk=len(kbs): e.matmul(
                                accv[:, g, :], lhsT=PTs[k2][:, g * 128:(g + 1) * 128], rhs=Vs[:, kblk % 8, kv, 0:65],
                                start=(bi == 0 and g == 0), stop=(bi == nk - 1), skip_group_check=True),
                                reads=[BPTs[k2], BVs[(kblk // 4) % 2], BVones], writes=[Bbank[pa]])
                    ai = ctr["k2"] % 2
                    P.op("vector", lambda e, ai=ai, accv=accv, kv=kv: e.tensor_tensor(
                        out=dn[ai][:], in0=accv[:, :, 64], in1=esink[:, kv * 4:kv * 4 + 4], op=ALU.add),
                        reads=[Bbank[pa], Bes], writes=[Bdn[ai]])
                    P.op("vector", lambda e, ai=ai: e.reciprocal(out=dn[ai][:], in_=dn[ai][:]), writes=[Bdn[ai]])
                    P.op("vector", lambda e, ai=ai, accv=accv: e.tensor_tensor(
                        out=att[ai][:], in0=accv[:, :, 0:64], in1=dn[ai][:, :].unsqueeze(2).to_broadcast([128, 4, 64]),
                        op=ALU.mult),
                        reads=[Bbank[pa], Bdn[ai]], writes=[Batt[ai]])
                    for p2 in range(2):
                        pb = nb_(MISC, "pm")
                        pt = bank[pb].bitcast(BF16)[:, 0:128]
                        P.op("tensor", lambda e, ai=ai, p2=p2, pt=pt: e.transpose(
                            out=pt, in_=att[ai][:, p2 * 2:p2 * 2 + 2, :].rearrange("p g d -> p (g d)"), identity=IDB),
                            reads=[Batt[ai], Bcbf], writes=[Bbank[pb]])
                        P.op("vector", lambda e, pt=pt, kv=kv, p2=p2, r=r: e.tensor_copy(
                            out=mo[mi][:, kv * 2 + p2, r * 128:(r + 1) * 128], in_=pt),
                            reads=[Bbank[pb]], writes=[Bmo[mi]])
            for r in range(4 if STAGE >= 5 else 0):
                c = t * 4 + r
                for m in range(2):
                    ps = nb_(SC, "sc")
                    P.op("tensor", lambda e, ps=ps, m=m, r=r: e.matmul(
                        bank[ps][:, 0:128], lhsT=kmT[:, m, r * 128:(r + 1) * 128], rhs=qmT[:, m, r * 128:(r + 1) * 128],
                        start=True, stop=True), reads=[Bkmt, Bqm], writes=[Bbank[ps]])
                    k2 = ctr["k2"] % 2
                    ctr["k2"] += 1
                    P.op("vector", lambda e, ps=ps, k2=k2, c=c, m=m: e.scalar_tensor_tensor(
                        out=qkT[k2][:], in0=bank[ps][:, 0:128], scalar=wsc[:, c, m:m + 1], in1=CAUS,
                        op0=ALU.mult, op1=ALU.mult),
                        reads=[Bbank[ps], Bcbf] + gate_reads, writes=[BqkT[k2]])
                    P.op("vector", lambda e, k2=k2, c=c, m=m, r=r: e.tensor_scalar(
                        out=kw[k2][:], in0=km[:, r, m, :], scalar1=w_[:, c, m:m + 1], scalar2=0.0,
                        op0=ALU.mult, op1=ALU.add),
                        reads=[Bkm] + gate_reads, writes=[Bkw[k2]])
                    pn = nb_(ACC, "acc")
                    P.op("tensor", lambda e, pn=pn, k2=k2, r=r, m=m: e.matmul(
                        bank[pn][:, 0:257], lhsT=qkT[k2][:], rhs=vm[:, r, m, 0:257], start=True, stop=False),
                        reads=[BqkT[k2], Bvm], writes=[Bbank[pn]])
                    P.op("tensor", lambda e, pn=pn, r=r, m=m: e.matmul(
                        bank[pn][:, 0:257], lhsT=qmT[:, m, r * 128:(r + 1) * 128], rhs=Ct[m][:, 0:257],
                        start=False, stop=True),
                        reads=[Bqm, BCt[m]], writes=[Bbank[pn]])
                    pc = nb_(ACC, "acc")
                    P.op("tensor", lambda e, pc=pc, k2=k2, r=r, m=m: e.matmul(
                        bank[pc][:, 0:257], lhsT=kw[k2][:], rhs=vm[:, r, m, 0:257], start=True, stop=True),
                        reads=[Bkw[k2], Bvm], writes=[Bbank[pc]])
                    P.op("vector", lambda e, pc=pc, c=c, m=m: e.tensor_scalar(
                        out=tC[m][:], in0=bank[pc][:, 0:257], scalar1=scl[:, c, m:m + 1], scalar2=0.0,
                        op0=ALU.mult, op1=ALU.add), reads=[Bbank[pc]], writes=[BtC[m]])
                    P.op("vector", lambda e, c=c, m=m: e.scalar_tensor_tensor(
                        out=Cst[m][:], in0=Cst[m][:], scalar=dec[:, c, m:m + 1], in1=tC[m][:],
                        op0=ALU.mult, op1=ALU.add), reads=[BtC[m]], writes=[BC[m]])
                    if c + 1 < NB:
                        P.op("vector", lambda e, c=c, m=m: e.tensor_scalar(
                            out=Ct[m][:, 0:257], in0=Cst[m][:], scalar1=ints[:, c + 1, m:m + 1], scalar2=0.0,
                            op0=ALU.mult, op1=ALU.add), reads=[BC[m]], writes=[BCt[m]])
                    hi = ctr["hh"] % 2
                    ctr["hh"] += 1
                    P.op("vector", lambda e, pn=pn, hi=hi, c=c, m=m: e.tensor_scalar(
                        out=dd[hi][:, 0:1], in0=bank[pn][:, 256:257], scalar1=-1.0, scalar2=bnd[:, c, m:m + 1],
                        op0=ALU.mult, op1=ALU.max), reads=[Bbank[pn]], writes=[Bdd[hi]])
                    P.op("vector", lambda e, pn=pn, hi=hi: e.tensor_tensor(
                        out=dd[hi][:, 0:1], in0=dd[hi][:, 0:1], in1=bank[pn][:, 256:257], op=ALU.max),
                        reads=[Bbank[pn]], writes=[Bdd[hi]])
                    P.op("vector", lambda e, hi=hi: e.reciprocal(out=dd[hi][:, 1:2], in_=dd[hi][:, 0:1]),
                         writes=[Bdd[hi]])
                    P.op("vector", lambda e, pn=pn, hi=hi: e.tensor_scalar(
                        out=hh[hi][:], in0=bank[pn][:, 0:256], scalar1=dd[hi][:, 1:2], scalar2=0.0,
                        op0=ALU.mult, op1=ALU.add), reads=[Bbank[pn], Bdd[hi]], writes=[Bhh[hi]])
                    P.op("scalar", lambda e, hi=hi: e.activation(out=hj[:], in_=hh[hi][:], func=AF.Square,
                                                                 accum_out=ssq[:, hi:hi + 1]),
                         reads=[Bhh[hi]], writes=[Bhj, Bssq])
                    P.op("scalar", lambda e, hi=hi: e.activation(out=rstd[:, hi:hi + 1], in_=ssq[:, hi:hi + 1],
                                                                 func=AF.Sqrt, scale=1.0 / 256.0, bias=epsc[:, 0:1]),
                         reads=[Beps], writes=[Bssq])
                    P.op("vector", lambda e, hi=hi: e.reciprocal(out=rstd[:, hi:hi + 1], in_=rstd[:, hi:hi + 1]),
                         writes=[Bssq])
                    P.op("vector", lambda e, hi=hi, m=m: e.scalar_tensor_tensor(
                        out=hn[hi][:], in0=hh[hi][:], scalar=rstd[:, hi:hi + 1], in1=GAIN(m),
                        op0=ALU.mult, op1=ALU.mult), reads=[Bhh[hi], Bssq, Bvec], writes=[Bhn[hi]])
                    P.op("vector", lambda e, hi=hi, r=r, m=m: e.tensor_tensor(
                        out=mem[hi][:], in0=hn[hi][:], in1=osig[:, r, m, :], op=ALU.mult),
                        reads=[Bhn[hi], Bos], writes=[Bmem[hi]])
                    for p2 in range(2):
                        pb = nb_(MISC, "pm")
                        pt = bank[pb].bitcast(BF16)[:, 0:128]
                        P.op("tensor", lambda e, hi=hi, p2=p2, pt=pt: e.transpose(
                            out=pt, in_=mem[hi][:, p2 * 128:(p2 + 1) * 128], identity=IDB),
                            reads=[Bmem[hi], Bcbf], writes=[Bbank[pb]])
                        P.op("vector", lambda e, pt=pt, m=m, p2=p2, r=r: e.tensor_copy(
                            out=mo[mi][:, 4 + m * 2 + p2, r * 128:(r + 1) * 128], in_=pt),
                            reads=[Bbank[pb]], writes=[Bmo[mi]])
            evs.append(P.dma("sync", [lambda e, t=t, mi=mi: e.dma_start(
                out=out_d.rearrange("(q p) n -> p q n", p=128)[:, :, t * 512:(t + 1) * 512], in_=mo[mi][:])],
                Bmo[mi], reads=[Bmo[mi]]))
        for t in range(NT):
            tile(t)
        P.finish(evs)
        P.emit()
    return nc


def consts_Aeven(hf):
    import ml_dtypes
    U = np.triu(np.ones((128, 128), np.float32))
    ones = np.ones((128, 128), np.float32)
    I = np.eye(128, dtype=np.float32)
    c32 = np.stack([U, ones, I])
    caus = np.triu(np.ones((128, 128), np.float32))
    cbf = np.stack([I, caus]).astype(ml_dtypes.bfloat16)
    k = np.arange(128)[:, None]
    q = np.arange(128)[None, :]
    tab = np.zeros((128, 4, 4, 128), np.float32)
    for kb in range(2):
        dist = (q + 128 - k) if kb == 0 else (q - k)
        valid = (dist >= 0) & (dist < 128)
        for kv in range(2):
            for g in range(4):
                hq = hf * 8 + kv * 4 + g
                slope = 2.0 ** (-8.0 * (hq + 1) / 16.0)
                tab[:, kb * 2 + kv, g, :] = np.where(valid, -slope * dist, -30000.0)
    return dict(c32=c32, cbf=cbf, tab=tab.reshape(128, 4, 512).astype(np.float32))


import ml_dtypes

NCORES = 8
SEQ = 8192
NTOK = 4096
_PROGS = {}


def _prog(name):
    if name not in _PROGS:
        if name == "P0":
            _PROGS[name] = build_B(NTOK, T=1024, HG=4, NS=3, has_mix=False, do_ffn=False, write_h=False)
        elif name == "B":
            _PROGS[name] = build_B(NTOK, T=1024, HG=4, NS=3)
        elif name == "Blast":
            _PROGS[name] = build_B(NTOK, T=1024, HG=4, NS=3, last=True)
        elif name == "Ae":
            _PROGS[name] = build_Aeven(SEQ)
        elif name == "Ao":
            _PROGS[name] = build_Aodd(SEQ, 8)
    return _PROGS[name]


def _run(name, in_maps):
    res = run_bass_kernel_spmd(_prog(name), in_maps, core_ids=list(range(NCORES)))
    return res.results


def kernel(x, norm_mix, norm_ffn, norm_final, w_in_even, w_out_even, swa_sinks, mlstm_b_i, mlstm_b_f,
           mlstm_head_gain, w_in_odd, fox_b_f, w_out_odd, w_gate, w_up, w_down):
    f32 = np.float32
    A = lambda a: np.ascontiguousarray(np.asarray(a))
    x = np.asarray(x, f32)
    ident = np.eye(128, dtype=f32).astype(ml_dtypes.bfloat16)
    cores = [(c // 2, c % 2) for c in range(NCORES)]
    zeros_g = A(np.asarray(norm_mix[0], f32))
    in_maps = []
    for (b, hf) in cores:
        in_maps.append(dict(h_in=A(x[b, hf * NTOK:(hf + 1) * NTOK]), g_ffn=zeros_g, g_next=zeros_g, ident=ident))
    res = _run("P0", in_maps)
    h = [A(x[b, hf * NTOK:(hf + 1) * NTOK]) for (b, hf) in cores]
    xnT = [res[c]["xnT_out"] for c in range(NCORES)]
    out = None
    for layer in range(4):
        j = layer // 2
        xfull = [np.concatenate([xnT[2 * b], xnT[2 * b + 1]], axis=1) for b in range(4)]
        in_maps = []
        if layer % 2 == 0:
            w = np.asarray(w_in_even[j], f32)
            for (b, hf) in cores:
                qs = w[:, 512 * hf:512 * hf + 512]
                ks = w[:, 1024 + 128 * hf:1024 + 128 * hf + 128]
                vs = w[:, 1280 + 128 * hf:1280 + 128 * hf + 128]
                qm = w[:, 1536 + 256 * hf:1536 + 256 * hf + 256]
                kmm = w[:, 2048 + 256 * hf:2048 + 256 * hf + 256]
                vmm = w[:, 2560 + 512 * hf:2560 + 512 * hf + 512]
                om = w[:, 3584 + 512 * hf:3584 + 512 * hf + 512]
                ig = w[:, 4608 + 2 * hf:4608 + 2 * hf + 2]
                fg = w[:, 4612 + 2 * hf:4612 + 2 * hf + 2]
                wfm = A(np.concatenate([qs, ks, qm, kmm], axis=1))
                wtm = A(np.concatenate([vs, kmm, ig, fg, vmm, om], axis=1))
                vec = A(np.concatenate([np.asarray(swa_sinks[j], f32)[8 * hf:8 * hf + 8],
                                        np.asarray(mlstm_b_i[j], f32)[2 * hf:2 * hf + 2],
                                        np.asarray(mlstm_b_f[j], f32)[2 * hf:2 * hf + 2],
                                        np.asarray(mlstm_head_gain[j], f32)[512 * hf:512 * hf + 512]]))
                in_maps.append(dict(xnT=xfull[b], wfm=wfm, wtm=wtm, vec=vec, **consts_Aeven(hf)))
            res = _run("Ae", in_maps)
            mix_full = [np.concatenate([res[2 * b]["mixT"][0:512], res[2 * b + 1]["mixT"][0:512],
                                        res[2 * b]["mixT"][512:1024], res[2 * b + 1]["mixT"][512:1024]], axis=0)
                        for b in range(4)]
            wo = A(np.asarray(w_out_even[j], f32))
        else:
            w = np.asarray(w_in_odd[j], f32)
            for (b, hf) in cores:
                wq = A(w[:, 1024 * hf:1024 * hf + 1024])
                wk = A(w[:, 2048 + 1024 * hf:2048 + 1024 * hf + 1024])
                wv = w[:, 4096 + 1024 * hf:4096 + 1024 * hf + 1024].reshape(D, 8, 128)
                wf = w[:, 6144 + 8 * hf:6144 + 8 * hf + 8].reshape(D, 8, 1)
                wvf = A(np.concatenate([wv, wf], axis=2))
                bfv = A(np.asarray(fox_b_f[j], f32)[8 * hf:8 * hf + 8])
                in_maps.append(dict(xnT=xfull[b], wq=wq, wk=wk, wvf=wvf, bf=bfv, **consts_Aodd()))
            res = _run("Ao", in_maps)
            mix_full = [np.concatenate([res[2 * b]["mixT"], res[2 * b + 1]["mixT"]], axis=0) for b in range(4)]
            wo = A(np.asarray(w_out_odd[j], f32))
        last = layer == 3
        g_next = A(np.asarray(norm_final if last else norm_mix[layer + 1], f32))
        in_maps = []
        for ci, (b, hf) in enumerate(cores):
            in_maps.append(dict(h_in=h[ci], mixT=A(mix_full[b][:, hf * NTOK:(hf + 1) * NTOK]), wo=wo,
                                g_ffn=A(np.asarray(norm_ffn[layer], f32)), wg=A(np.asarray(w_gate[layer], f32)),
                                wu=A(np.asarray(w_up[layer], f32)), wd=A(np.asarray(w_down[layer], f32)),
                                g_next=g_next, ident=ident))
        res = _run("Blast" if last else "B", in_maps)
        h = [res[c]["h_out"] for c in range(NCORES)]
        if not last:
            xnT = [res[c]["xnT_out"] for c in range(NCORES)]
    out = np.zeros((4, SEQ, D), f32)
    for ci, (b, hf) in enumerate(cores):
        out[b, hf * NTOK:(hf + 1) * NTOK] = h[ci]
    return out
```
